# Optimizing a Trainium2 kernel written in Bass

```python
import math
import jax
import jax.numpy as jnp
from jax import lax
import numpy as np

D_MODEL = 1024
BATCH = 4
SEQ = 8192
DEPTH = 2

D_MIX = D_MODEL
POOL_W = D_MIX // 4
POOL_WINDOWS = (2, 4, 8, 16)
POOL_GROUP = POOL_W // len(POOL_WINDOWS)
DIFF_HEADS = 4
DIFF_QK_DIM = 32
DIFF_V_DIM = 2 * DIFF_QK_DIM
DIFF_W = DIFF_HEADS * DIFF_V_DIM
Q_BLOCK = 128
SSD_INNER = D_MIX - POOL_W - DIFF_W
SSD_HEAD_DIM = 64
SSD_HEADS = SSD_INNER // SSD_HEAD_DIM
SSD_GROUPS = 2
SSD_HEADS_PER_GROUP = SSD_HEADS // SSD_GROUPS
SSD_STATE = 128
CONV_K = 4
CONV_CH = SSD_INNER + 2 * SSD_GROUPS * SSD_STATE
CHUNK = 128
D_FF = 4 * D_MODEL
PLE_DIM = 256
ALPHA = (2 * DEPTH) ** 0.25
BETA = (8 * DEPTH) ** -0.25
LN_EPS = 1e-5
IN_SIZES = (POOL_W, DIFF_HEADS * 2 * DIFF_QK_DIM, DIFF_HEADS * 2 * DIFF_QK_DIM, DIFF_W,
            SSD_INNER, CONV_CH, SSD_HEADS)
IN_W = sum(IN_SIZES)

kernel_name = 'hymba_style_pool_diffattn_ssd_deepnorm'


def layer_norm(x, g, b):
    xf = x.astype(jnp.float32)
    mu = jnp.mean(xf, axis=-1, keepdims=True)
    xc = xf - mu
    var = jnp.mean(xc * xc, axis=-1, keepdims=True)
    y = xc * lax.rsqrt(var + LN_EPS) * g.astype(jnp.float32) + b.astype(jnp.float32)
    return y.astype(x.dtype)


def rms_norm(x, g):
    xf = x.astype(jnp.float32)
    y = xf * lax.rsqrt(jnp.mean(xf * xf, axis=-1, keepdims=True) + LN_EPS)
    return y * g.astype(jnp.float32)


def split_cols(a, sizes):
    out = []
    o = 0
    for s in sizes:
        out.append(a[..., o:o + s])
        o += s
    return out


def multiscale_pool(u, w_grp, scale):
    Bsz, S, _ = u.shape
    uf = u.astype(jnp.float32)
    cs = jnp.cumsum(uf, axis=1)
    pos = jnp.arange(1, S + 1, dtype=jnp.float32)
    outs = []
    for gi, w in enumerate(POOL_WINDOWS):
        sl = slice(gi * POOL_GROUP, (gi + 1) * POOL_GROUP)
        csg = cs[..., sl]
        lag = jnp.pad(csg, ((0, 0), (w, 0), (0, 0)))[:, :S]
        cnt = jnp.minimum(pos, float(w))[None, :, None]
        outs.append((csg - lag) / cnt - uf[..., sl])
    pooled = jnp.stack(outs, axis=2)
    mixed = jnp.einsum('bsgc,gcd->bsgd', pooled, w_grp.astype(jnp.float32)).reshape(Bsz, S, POOL_W)
    return (mixed * scale.astype(jnp.float32)).astype(u.dtype)


def diff_attention(q, k, v, lam_q1, lam_k1, lam_q2, lam_k2, norm_g, lambda_init):
    Bsz, S, _ = q.shape
    dtype = q.dtype
    q = q.reshape(Bsz, S, DIFF_HEADS, 2, DIFF_QK_DIM).transpose(0, 2, 3, 1, 4) * (DIFF_QK_DIM ** -0.5)
    k = k.reshape(Bsz, S, DIFF_HEADS, 2, DIFF_QK_DIM).transpose(0, 2, 3, 1, 4)
    vf = v.reshape(Bsz, S, DIFF_HEADS, DIFF_V_DIM).transpose(0, 2, 1, 3).astype(jnp.float32)
    f32 = jnp.float32
    lam = (jnp.exp(jnp.sum(lam_q1.astype(f32) * lam_k1.astype(f32)))
           - jnp.exp(jnp.sum(lam_q2.astype(f32) * lam_k2.astype(f32))) + lambda_init)
    nblk = S // Q_BLOCK
    qb = q.reshape(Bsz, DIFF_HEADS, 2, nblk, Q_BLOCK, DIFF_QK_DIM).transpose(3, 0, 1, 2, 4, 5)
    kpos = jnp.arange(S)

    def block(args):
        qi, bi = args
        s = jnp.einsum('bhjqd,bhjkd->bhjqk', qi, k).astype(f32)
        qpos = bi * Q_BLOCK + jnp.arange(Q_BLOCK)
        mask = kpos[None, :] <= qpos[:, None]
        s = jnp.where(mask, s, -jnp.inf)
        a = jax.nn.softmax(s, axis=-1)
        w = a[:, :, 0] - lam * a[:, :, 1]
        return jnp.einsum('bhqk,bhkv->bhqv', w, vf)

    o = lax.map(block, (qb, jnp.arange(nblk)))
    o = o.transpose(1, 0, 3, 2, 4).reshape(Bsz, S, DIFF_HEADS, DIFF_V_DIM)
    o = rms_norm(o, norm_g) * (1.0 - lambda_init)
    return o.reshape(Bsz, S, DIFF_W).astype(dtype)


def causal_depthwise_conv(x, w, b):
    C = x.shape[-1]
    y = lax.conv_general_dilated(x, w[:, None, :].astype(x.dtype), window_strides=(1,),
                                 padding=((CONV_K - 1, 0),), dimension_numbers=('NWC', 'WIO', 'NWC'),
                                 feature_group_count=C)
    return y + b.astype(x.dtype)


def ssd_mixer(z, xbc, dt_raw, conv_w, conv_b, dt_bias, a_log, d_skip, norm_g):
    Bsz, S, _ = xbc.shape
    dtype = xbc.dtype
    f32 = jnp.float32
    G, R, P, N, Lc = SSD_GROUPS, SSD_HEADS_PER_GROUP, SSD_HEAD_DIM, SSD_STATE, CHUNK
    nc = S // Lc
    xbc = jax.nn.silu(causal_depthwise_conv(xbc, conv_w, conv_b))
    xs, Bm, Cm = split_cols(xbc, (SSD_INNER, G * N, G * N))
    xs = xs.astype(f32).reshape(Bsz, nc, Lc, G, R, P)
    Bm = Bm.astype(f32).reshape(Bsz, nc, Lc, G, N)
    Cm = Cm.astype(f32).reshape(Bsz, nc, Lc, G, N)
    dt = jax.nn.softplus(dt_raw.astype(f32) + dt_bias.astype(f32)).reshape(Bsz, nc, Lc, G, R)
    A = -jnp.exp(a_log.astype(f32)).reshape(G, R)
    xdt = xs * dt[..., None]
    acs = jnp.cumsum(dt * A, axis=2).transpose(0, 1, 3, 4, 2)
    seg = acs[..., :, None] - acs[..., None, :]
    tril = jnp.tril(jnp.ones((Lc, Lc), dtype=bool))
    Lmat = jnp.exp(jnp.where(tril, seg, -jnp.inf))
    CB = jnp.einsum('bclgn,bcsgn->bcgls', Cm, Bm)
    y_diag = jnp.einsum('bcgls,bcgrls,bcsgrp->bclgrp', CB, Lmat, xdt)
    decay_states = jnp.exp(acs[..., -1:] - acs)
    states = jnp.einsum('bclgn,bcgrl,bclgrp->bcgrpn', Bm, decay_states, xdt)
    chunk_decay = jnp.exp(acs[..., -1])

    def step(h, inp):
        st, dec = inp
        return h * dec[..., None, None] + st, h

    h0 = jnp.zeros((Bsz, G, R, P, N), f32)
    _, prev = lax.scan(step, h0, (jnp.moveaxis(states, 1, 0), jnp.moveaxis(chunk_decay, 1, 0)))
    prev = jnp.moveaxis(prev, 0, 1)
    y_off = jnp.einsum('bclgn,bcgrpn,bcgrl->bclgrp', Cm, prev, jnp.exp(acs))
    y = y_diag + y_off + xs * d_skip.astype(f32).reshape(G, R)[:, :, None]
    y = y.reshape(Bsz, S, SSD_INNER) * jax.nn.silu(z.astype(f32))
    return rms_norm(y, norm_g).astype(dtype)


def setup_inputs(seed: int = 0) -> dict:
    key = jax.random.key(seed)
    ks = jax.random.split(key, 32)
    L = DEPTH
    f32 = jnp.float32

    def nrm(k, shape):
        return jax.random.normal(k, shape, f32)

    x = nrm(ks[0], (BATCH, SEQ, D_MODEL))
    p = nrm(ks[1], (DEPTH, BATCH, SEQ, PLE_DIM))
    ln_in_g = 1.0 + 0.02 * nrm(ks[2], (D_MODEL,))
    ln_in_b = 0.02 * nrm(ks[3], (D_MODEL,))
    w_in = nrm(ks[4], (L, D_MODEL, IN_W)) * (D_MODEL ** -0.5)
    pool_w = nrm(ks[5], (L, len(POOL_WINDOWS), POOL_GROUP, POOL_GROUP)) * (POOL_GROUP ** -0.5)
    pool_scale = 1.0 + 0.02 * nrm(ks[6], (L, POOL_W))
    lam_q1 = 0.1 * nrm(ks[7], (L, DIFF_QK_DIM))
    lam_k1 = 0.1 * nrm(ks[8], (L, DIFF_QK_DIM))
    lam_q2 = 0.1 * nrm(ks[9], (L, DIFF_QK_DIM))
    lam_k2 = 0.1 * nrm(ks[10], (L, DIFF_QK_DIM))
    diff_norm_g = 1.0 + 0.02 * nrm(ks[11], (L, DIFF_V_DIM))
    conv_w = nrm(ks[12], (L, CONV_K, CONV_CH)) * (CONV_K ** -0.5)
    conv_b = 0.02 * nrm(ks[13], (L, CONV_CH))
    dt0 = jnp.exp(jax.random.uniform(ks[14], (L, SSD_HEADS), f32, minval=math.log(1e-3), maxval=math.log(1e-1)))
    dt_bias = dt0 + jnp.log(-jnp.expm1(-dt0))
    a_log = jnp.log(jax.random.uniform(ks[15], (L, SSD_HEADS), f32, minval=1.0, maxval=16.0))
    d_skip = 1.0 + 0.02 * nrm(ks[16], (L, SSD_HEADS))
    ssd_norm_g = 1.0 + 0.02 * nrm(ks[17], (L, SSD_INNER))
    w_out = nrm(ks[18], (L, D_MIX, D_MODEL)) * (D_MIX ** -0.5 * BETA)
    ln1_g = 1.0 + 0.02 * nrm(ks[19], (L, D_MODEL))
    ln1_b = 0.02 * nrm(ks[20], (L, D_MODEL))
    w_ff1 = nrm(ks[21], (L, D_MODEL, D_FF)) * (D_MODEL ** -0.5)
    w_ff2 = nrm(ks[22], (L, D_FF, D_MODEL)) * (D_FF ** -0.5 * BETA)
    w_ple = nrm(ks[23], (L, PLE_DIM, D_MODEL)) * (PLE_DIM ** -0.5 * BETA)
    w_ple_gate = nrm(ks[24], (L, D_MODEL, D_MODEL)) * (D_MODEL ** -0.5)
    ln2_g = 1.0 + 0.02 * nrm(ks[25], (L, D_MODEL))
    ln2_b = 0.02 * nrm(ks[26], (L, D_MODEL))
    return {'x': x, 'p': p, 'ln_in_g': ln_in_g, 'ln_in_b': ln_in_b, 'w_in': w_in,
            'pool_w': pool_w, 'pool_scale': pool_scale,
            'lam_q1': lam_q1, 'lam_k1': lam_k1, 'lam_q2': lam_q2, 'lam_k2': lam_k2, 'diff_norm_g': diff_norm_g,
            'conv_w': conv_w, 'conv_b': conv_b, 'dt_bias': dt_bias, 'a_log': a_log, 'd_skip': d_skip,
            'ssd_norm_g': ssd_norm_g, 'w_out': w_out, 'ln1_g': ln1_g, 'ln1_b': ln1_b,
            'w_ff1': w_ff1, 'w_ff2': w_ff2, 'w_ple': w_ple, 'w_ple_gate': w_ple_gate,
            'ln2_g': ln2_g, 'ln2_b': ln2_b}


def reference(x, p, ln_in_g, ln_in_b, w_in, pool_w, pool_scale, lam_q1, lam_k1, lam_q2, lam_k2, diff_norm_g,
              conv_w, conv_b, dt_bias, a_log, d_skip, ssd_norm_g, w_out, ln1_g, ln1_b,
              w_ff1, w_ff2, w_ple, w_ple_gate, ln2_g, ln2_b):
    h = layer_norm(x, ln_in_g, ln_in_b)
    for i in range(DEPTH):
        lambda_init = 0.8 - 0.6 * math.exp(-0.3 * i)
        proj = h @ w_in[i]
        u_pool, q, k, v, z, xbc, dt_raw = split_cols(proj, IN_SIZES)
        o_pool = multiscale_pool(u_pool, pool_w[i], pool_scale[i])
        o_diff = diff_attention(q, k, v, lam_q1[i], lam_k1[i], lam_q2[i], lam_k2[i], diff_norm_g[i], lambda_init)
        o_ssd = ssd_mixer(z, xbc, dt_raw, conv_w[i], conv_b[i], dt_bias[i], a_log[i], d_skip[i], ssd_norm_g[i])
        mix = jnp.concatenate([o_pool, o_diff, o_ssd], axis=-1) @ w_out[i]
        h = layer_norm(ALPHA * h + mix, ln1_g[i], ln1_b[i])
        ff = jnp.square(jax.nn.relu(h @ w_ff1[i])) @ w_ff2[i]
        ple = (p[i] @ w_ple[i]) * jax.nn.sigmoid(h @ w_ple_gate[i])
        h = layer_norm(ALPHA * h + ff + ple, ln2_g[i], ln2_b[i])
    return h
```

```python
import math
from contextlib import ExitStack

import numpy as np
import ml_dtypes

import concourse.bass as bass
import concourse.mybir as mybir
from concourse.bass_utils import run_bass_kernel_spmd

F32 = mybir.dt.float32
BF16 = mybir.dt.bfloat16
AF = mybir.ActivationFunctionType
ALU = mybir.AluOpType
AX = mybir.AxisListType

D = 1024
SEQ = 8192
NB = 4
TOK = 4096
NT = TOK // 128
NCH = SEQ // 128
HW = 1280
ALPHA = 4 ** 0.25
LN_EPS = 1e-5
DEPTH = 2


class KB:
    EPOCH = 10 ** 9

    def __init__(self):
        self.nc = bass.Bass("TRN2", target_bir_lowering=False)
        nc = self.nc
        self.root = ExitStack()
        self.eng = {'pe': nc.tensor, 'act': nc.scalar, 'dve': nc.vector, 'pool': nc.gpsimd, 'sp': nc.sync}
        self.nsem = 0
        self.cnt = {e: 0 for e in self.eng}
        self.sem = {e: self._newsem(e) for e in self.eng}
        self.waited = {e: {} for e in self.eng}
        self.track = {}
        self.dma_pool = {}
        self.dma_rr = {}
        for e, n in (('sp', 16), ('pool', 8), ('act', 8)):
            self.dma_pool[e] = [[self._newsem('d' + e), 0] for _ in range(n)]
            self.dma_rr[e] = 0
        self.ninst = 0

    def _newsem(self, tag):
        self.nsem += 1
        return self.root.enter_context(self.nc.semaphore(f"s_{tag}_{self.nsem}"))

    def sbuf(self, stack, name, shape, dtype):
        return stack.enter_context(self.nc.sbuf_tensor(name, list(shape), dtype))

    def psum(self, stack, name, shape, dtype=F32):
        return stack.enter_context(self.nc.psum_tensor(name, list(shape), dtype))

    def _wait(self, e, tok):
        sem, val, src = tok
        if src == e and e == 'pe':
            return
        w = self.waited[e]
        k = id(sem)
        if w.get(k, 0) >= val:
            return
        self.eng[e].wait_ge(sem, val)
        w[k] = val

    def _deps(self, e, reads, writes):
        for k in reads:
            t = self.track.get(k)
            if t is not None and t[0] is not None:
                self._wait(e, t[0])
            if t is not None and isinstance(k, str) and k.startswith('ps'):
                for src, r in t[1].items():
                    if src != e:
                        self._wait(e, r)
        for k in writes:
            t = self.track.get(k)
            if t is not None:
                if t[0] is not None:
                    self._wait(e, t[0])
                for r in t[1].values():
                    self._wait(e, r)
                for r in t[2]:
                    self._wait(e, r)

    def _commit(self, tok, reads, writes):
        for k in reads:
            t = self.track.get(k)
            if t is None:
                t = self.track[k] = [None, {}, []]
            if tok[2] == 'dma':
                t[2].append(tok)
            else:
                t[1][tok[2]] = tok
        for k in writes:
            self.track[k] = [tok, {}, []]

    def op(self, e, fn, reads=(), writes=()):
        self._deps(e, reads, writes)
        if self.cnt[e] >= self.EPOCH:
            self.sem[e] = self._newsem(e)
            self.cnt[e] = 0
        ins = fn(self.eng[e])
        self.cnt[e] += 1
        self.ninst += 1
        ins.then_inc(self.sem[e], 1)
        tok = (self.sem[e], self.cnt[e], e)
        self._commit(tok, reads, writes)
        return tok

    def dma(self, e, out, in_, reads=(), writes=()):
        pool = self.dma_pool[e]
        slot = pool[self.dma_rr[e] % len(pool)]
        self.dma_rr[e] += 1
        sem, val = slot
        if val > 0:
            self._wait(e, (sem, val, 'dma'))
        self._deps(e, reads, writes)
        self.eng[e].dma_start(out=out, in_=in_).then_inc(sem, 16)
        self.ninst += 1
        slot[1] = val + 16
        tok = (sem, val + 16, 'dma')
        self._commit(tok, reads, writes)
        return tok

    def collective(self, kind, ins, outs, groups, reads=(), writes=()):
        e = 'pool'
        self._deps(e, reads, writes)
        sem = self._newsem('cc')
        self.nc.gpsimd.collective_compute(kind, ALU.bypass, replica_groups=groups, ins=ins, outs=outs).then_inc(sem)
        tok = (sem, 1, 'dma')
        self._commit(tok, reads, writes)
        return tok

    def all_tokens(self):
        toks = [(self.sem[x], self.cnt[x], x) for x in self.eng if self.cnt[x] > 0]
        for pool in self.dma_pool.values():
            for sem, val in pool:
                if val > 0:
                    toks.append((sem, val, 'dma'))
        return toks

    def barrier(self):
        toks = self.all_tokens()
        for t in self.track.values():
            for r in t[2]:
                if r[1] == 1 and r not in toks:
                    toks.append(r)
            if t[0] is not None and t[0][2] == 'dma' and t[0] not in toks:
                toks.append(t[0])
        for e in self.eng:
            for t in toks:
                self._wait(e, t)
        self.track.clear()

    def finish(self):
        toks = self.all_tokens()
        for t in self.track.values():
            if t[0] is not None and t[0][2] == 'dma' and t[0] not in toks:
                toks.append(t[0])
        for t in toks:
            self._wait('sp', t)
        self.root.close()


def bcast_rows(ap_row, n=128):
    return ap_row.partition_broadcast(n) if hasattr(ap_row, 'partition_broadcast') else ap_row.broadcast_to([n, ap_row.shape[-1]])


def emit_layernorm(kb, tag, x, out, g_t, b_t, eps_t, st, mv, rs, gk='A_g', bk='A_b', width=1024):
    xk, ok, stk = tag + ('x',), tag + ('o',), tag + ('st',)
    nh = width // 512
    for i in range(nh):
        kb.op('dve', lambda en, i=i: en.bn_stats(out=st[:, i, :], in_=x[:, i * 512:(i + 1) * 512]), reads=[xk], writes=[stk + (i,)])
    kb.op('dve', lambda en: en.bn_aggr(out=mv[:, :], in_=st[:, :, :]), reads=[stk + (i,) for i in range(nh)], writes=[tag + ('mv',)])
    kb.op('act', lambda en: en.activation(out=rs[:, 0:1], in_=mv[:, 1:2], func=AF.Sqrt, bias=eps_t[:, 0:1], scale=1.0),
          reads=[tag + ('mv',), 'eps'], writes=[tag + ('rs',)])
    kb.op('dve', lambda en: en.reciprocal(out=rs[:, 1:2], in_=rs[:, 0:1]), reads=[tag + ('rs',)], writes=[tag + ('rs2',)])
    kb.op('dve', lambda en: en.tensor_scalar(out=out, in0=x, scalar1=mv[:, 0:1], scalar2=rs[:, 1:2], op0=ALU.subtract, op1=ALU.mult),
          reads=[xk, tag + ('mv',), tag + ('rs2',)], writes=[ok])
    kb.op('pool', lambda en: en.tensor_tensor(out=out, in0=out, in1=g_t, op=ALU.mult), reads=[ok, gk], writes=[ok])
    kb.op('pool', lambda en: en.tensor_tensor(out=out, in0=out, in1=b_t, op=ALU.add), reads=[ok, bk], writes=[ok])


def transpose_tile(kb, src_bf, srck, ident, psb, psk, dst, dstk, nchunk, evac_eng):
    for k in range(nchunk):
        b = k // 4
        kb.op('pe', lambda en, k=k, b=b: en.matmul(psb[b][:, (k % 4) * 128:(k % 4 + 1) * 128], lhsT=src_bf[:, k * 128:(k + 1) * 128],
                                                  rhs=ident[:, :], start=True, stop=True),
              reads=[srck, 'ident'], writes=[psk[b]])
    for b in range((nchunk + 3) // 4):
        n = min(4, nchunk - 4 * b)
        eng = evac_eng[b % len(evac_eng)]
        if eng == 'act':
            kb.op('act', lambda en, b=b, n=n: en.activation(out=dst[:, 4 * b:4 * b + n, :], in_=psb[b][:, 0:n * 128].rearrange("p (a t) -> p a t", t=128), func=AF.Copy),
                  reads=[psk[b]], writes=[dstk])
        else:
            kb.op('dve', lambda en, b=b, n=n: en.tensor_copy(out=dst[:, 4 * b:4 * b + n, :], in_=psb[b][:, 0:n * 128].rearrange("p (a t) -> p a t", t=128)),
                  reads=[psk[b]], writes=[dstk])


def phase_A(kb, first, hin, hout, w_in_d, lng_d, lnb_d, ident_d, L_d, S_d, dt_d):
    nc = kb.nc
    with ExitStack() as st:
        W = kb.sbuf(st, "A_W", [128, 8, 2568], BF16)
        ident = kb.sbuf(st, "A_ident", [128, 128], BF16)
        eps_t = kb.sbuf(st, "A_eps", [128, 1], F32)
        g_t = kb.sbuf(st, "A_g", [128, 1024], F32)
        b_t = kb.sbuf(st, "A_b", [128, 1024], F32)
        xt = [kb.sbuf(st, f"A_x{i}", [128, 1024], F32) for i in range(2)]
        ht = [kb.sbuf(st, f"A_h{i}", [128, 1024], F32) for i in range(2)]
        hb = [kb.sbuf(st, f"A_hb{i}", [128, 1024], BF16) for i in range(2)]
        hT = [kb.sbuf(st, f"A_hT{i}", [128, 8, 128], BF16) for i in range(2)]
        pj = [kb.sbuf(st, f"A_pj{i}", [128, 2560], BF16) for i in range(2)]
        dtt = [kb.sbuf(st, f"A_dt{i}", [128, 8], F32) for i in range(2)]
        stt = [kb.sbuf(st, f"A_st{i}", [128, 2, 6], F32) for i in range(2)]
        mv = [kb.sbuf(st, f"A_mv{i}", [128, 2], F32) for i in range(2)]
        rs = [kb.sbuf(st, f"A_rs{i}", [128, 2], F32) for i in range(2)]
        ps = [kb.psum(st, f"A_ps{i}", [128, 512]) for i in range(8)]

        kb.dma('pool', out=ident[:, :], in_=ident_d, writes=['ident'])
        kb.op('dve', lambda en: en.memset(eps_t[:, :], LN_EPS), writes=['eps'])
        if first:
            kb.dma('sp', out=g_t[:, :], in_=lng_d.partition_broadcast(128), writes=['A_g'])
            kb.dma('sp', out=b_t[:, :], in_=lnb_d.partition_broadcast(128), writes=['A_b'])
        for k in range(8):
            kb.dma('pool', out=W[:, k, :], in_=w_in_d[k * 128:(k + 1) * 128, :], writes=[('W', k)])
        Wk = [('W', k) for k in range(8)]

        for i in range(NT):
            p = i % 2
            rows = slice(i * 128, (i + 1) * 128)
            if first:
                kb.dma('sp', out=xt[p][:, :], in_=hin[rows, :], writes=[('A', p, 'x')])
                emit_layernorm(kb, ('A', p), xt[p][:, :], ht[p][:, :], g_t[:, :], b_t[:, :], eps_t, stt[p], mv[p], rs[p])
                kb.dma('sp', out=hout[rows, :], in_=ht[p][:, :], reads=[('A', p, 'o')])
            else:
                kb.dma('sp', out=ht[p][:, :], in_=hin[rows, :], writes=[('A', p, 'o')])
            kb.op('act', lambda en, p=p: en.activation(out=hb[p][:, :], in_=ht[p][:, :], func=AF.Copy),
                  reads=[('A', p, 'o')], writes=[('A', p, 'hb')])
            transpose_tile(kb, hb[p], ('A', p, 'hb'), ident, [ps[0], ps[1]], ['ps0', 'ps1'], hT[p], ('A', p, 'hT'), 8, ['dve', 'act'])
            for cg in range(5):
                bank = 2 + (i * 5 + cg) % 5
                for k in range(8):
                    kb.op('pe', lambda en, k=k, cg=cg, bank=bank, p=p: en.matmul(ps[bank][:, :], lhsT=hT[p][:, k, :], rhs=W[:, k, cg * 512:(cg + 1) * 512],
                                                                               start=(k == 0), stop=(k == 7)),
                          reads=[('A', p, 'hT'), Wk[k]], writes=[f'ps{bank}'])
                if cg % 2 == 0:
                    kb.op('act', lambda en, cg=cg, bank=bank, p=p: en.activation(out=pj[p][:, cg * 512:(cg + 1) * 512], in_=ps[bank][:, :], func=AF.Copy),
                          reads=[f'ps{bank}'], writes=[('A', p, 'pj', cg)])
                else:
                    kb.op('dve', lambda en, cg=cg, bank=bank, p=p: en.tensor_copy(out=pj[p][:, cg * 512:(cg + 1) * 512], in_=ps[bank][:, :]),
                          reads=[f'ps{bank}'], writes=[('A', p, 'pj', cg)])
            for k in range(8):
                kb.op('pe', lambda en, k=k, p=p: en.matmul(ps[7][:, 0:8], lhsT=hT[p][:, k, :], rhs=W[:, k, 2560:2568], start=(k == 0), stop=(k == 7)),
                      reads=[('A', p, 'hT'), Wk[k]], writes=['ps7'])
            kb.op('dve', lambda en, p=p: en.tensor_copy(out=dtt[p][:, :], in_=ps[7][:, 0:8]), reads=['ps7'], writes=[('A', p, 'dt')])
            kb.dma('sp', out=L_d[rows, :], in_=pj[p][:, 0:1280], reads=[('A', p, 'pj', c) for c in range(3)])
            kb.dma('sp', out=S_d[rows, :], in_=pj[p][:, 1280:2560], reads=[('A', p, 'pj', c) for c in range(2, 5)])
            kb.dma('sp', out=dt_d[rows, :], in_=dtt[p][:, :], reads=[('A', p, 'dt')])
        kb.barrier()


def perm_cols(r):
    def half(q):
        return np.concatenate([np.arange(128 * q, 128 * q + 128), 256 + np.arange(128 * q, 128 * q + 128),
                               512 + np.arange(128 * q, 128 * q + 128), 768 + np.arange(128 * q, 128 * q + 128),
                               1024 + np.arange(256 * q, 256 * q + 256), 1536 + np.arange(256 * q, 256 * q + 256),
                               2048 + np.arange(128 * q, 128 * q + 128), 2304 + np.arange(128 * q, 128 * q + 128)])
    return np.concatenate([half(r), half(1 - r), 2560 + np.arange(4 * r, 4 * r + 4), 2560 + np.arange(4 * (1 - r), 4 * (1 - r) + 4)])


def build_A(first):
    kb = KB()
    nc = kb.nc
    hin = nc.dram_tensor("hin", [TOK, D], F32, kind="ExternalInput").ap()
    w_in = nc.dram_tensor("w_in", [D, 2568], F32, kind="ExternalInput").ap()
    lng = nc.dram_tensor("lng", [1, D], F32, kind="ExternalInput").ap() if first else None
    lnb = nc.dram_tensor("lnb", [1, D], F32, kind="ExternalInput").ap() if first else None
    ident = nc.dram_tensor("ident", [128, 128], F32, kind="ExternalInput").ap()
    hout = nc.dram_tensor("hout", [TOK, D], F32, kind="ExternalOutput").ap() if first else None
    L = nc.dram_tensor("L", [TOK, HW], BF16, kind="ExternalOutput").ap()
    S = nc.dram_tensor("S", [TOK, HW], BF16, kind="ExternalOutput").ap()
    dt = nc.dram_tensor("dt", [TOK, 8], F32, kind="ExternalOutput").ap()
    phase_A(kb, first, hin, hout, w_in, lng, lnb, ident, L, S, dt)
    kb.finish()
    return kb


def phase_C(kb, O_d, hres_d, h1_d, hout_d, p_d, w_out_d, ssdg_d, ln1g_d, ln1b_d, w_ff1_d, w_ff2_d, wg_d, wp_d, ln2g_d, ln2b_d, ident_d, ntile=NT, do1=True, do2=True):
    with ExitStack() as st:
      if do1:
          Wo = kb.sbuf(st, "C_Wo", [128, 8, 1024], BF16)
          ident = kb.sbuf(st, "C_ident", [128, 128], BF16)
          eps_t = kb.sbuf(st, "C_eps", [128, 1], F32)
          g_t = kb.sbuf(st, "C_g", [128, 1024], F32)
          b_t = kb.sbuf(st, "C_b", [128, 1024], F32)
          sg_t = kb.sbuf(st, "C_sg", [128, 512], F32)
          ot = [kb.sbuf(st, f"C_o{i}", [128, 1024], BF16) for i in range(2)]
          hr = [kb.sbuf(st, f"C_hr{i}", [128, 1024], F32) for i in range(2)]
          r1 = [kb.sbuf(st, f"C_r{i}", [128, 1024], F32) for i in range(2)]
          h1 = [kb.sbuf(st, f"C_h1{i}", [128, 1024], F32) for i in range(2)]
          oT = [kb.sbuf(st, f"C_oT{i}", [128, 8, 128], BF16) for i in range(2)]
          junk = kb.sbuf(st, "C_junk", [128, 256], F32)
          ss = [kb.sbuf(st, f"C_ss{i}", [128, 4], F32) for i in range(2)]
          stt = [kb.sbuf(st, f"C_st{i}", [128, 2, 6], F32) for i in range(2)]
          mv = [kb.sbuf(st, f"C_mv{i}", [128, 2], F32) for i in range(2)]
          rs = [kb.sbuf(st, f"C_rs{i}", [128, 2], F32) for i in range(2)]
          ps = [kb.psum(st, f"C_ps{i}", [128, 512]) for i in range(8)]

          kb.dma('pool', out=ident[:, :], in_=ident_d, writes=['ident'])
          kb.op('dve', lambda en: en.memset(eps_t[:, :], LN_EPS), writes=['eps'])
          kb.dma('sp', out=g_t[:, :], in_=ln1g_d.partition_broadcast(128), writes=['C_g'])
          kb.dma('sp', out=b_t[:, :], in_=ln1b_d.partition_broadcast(128), writes=['C_b'])
          kb.dma('sp', out=sg_t[:, :], in_=ssdg_d.partition_broadcast(128), writes=['C_sg'])
          for k in range(8):
              kb.dma('pool', out=Wo[:, k, :], in_=w_out_d[k * 128:(k + 1) * 128, :], writes=[('Wo', k)])
          for i in range(ntile):
              p = i % 2
              rows = slice(i * 128, (i + 1) * 128)
              kb.dma('sp', out=ot[p][:, :], in_=O_d[rows, :], writes=[('C', p, 'ot')])
              kb.dma('sp', out=hr[p][:, :], in_=hres_d[rows, :], writes=[('C', p, 'hr')])
              for hh in range(2):
                  cs = slice(256 + 512 * hh, 512 + 512 * hh)
                  kb.op('act', lambda en, cs=cs, hh=hh, p=p: en.activation(out=junk[:, :], in_=ot[p][:, cs], func=AF.Square, accum_out=ss[p][:, hh:hh + 1]),
                        reads=[('C', p, 'ot')], writes=['C_junk', ('C', p, 'ss', hh)])
              kb.op('dve', lambda en, p=p: en.tensor_tensor(out=ss[p][:, 2:3], in0=ss[p][:, 0:1], in1=ss[p][:, 1:2], op=ALU.add),
                    reads=[('C', p, 'ss', 0), ('C', p, 'ss', 1)], writes=[('C', p, 'ss', 2)])
              kb.op('act', lambda en, p=p: en.activation(out=ss[p][:, 3:4], in_=ss[p][:, 2:3], func=AF.Sqrt, bias=eps_t[:, 0:1], scale=1.0 / 512.0),
                    reads=[('C', p, 'ss', 2), 'eps'], writes=[('C', p, 'ss', 3)])
              kb.op('dve', lambda en, p=p: en.reciprocal(out=ss[p][:, 2:3], in_=ss[p][:, 3:4]), reads=[('C', p, 'ss', 3)], writes=[('C', p, 'ss', 2)])
              for hh in range(2):
                  cs = slice(256 + 512 * hh, 512 + 512 * hh)
                  kb.op('dve', lambda en, cs=cs, hh=hh, p=p: en.scalar_tensor_tensor(out=ot[p][:, cs], in0=ot[p][:, cs], scalar=ss[p][:, 2:3],
                                                                                     in1=sg_t[:, hh * 256:(hh + 1) * 256], op0=ALU.mult, op1=ALU.mult),
                        reads=[('C', p, 'ot'), ('C', p, 'ss', 2), 'C_sg'], writes=[('C', p, 'ot')])
              transpose_tile(kb, ot[p], ('C', p, 'ot'), ident, [ps[0], ps[1]], ['ps0', 'ps1'], oT[p], ('C', p, 'oT'), 8, ['dve', 'act'])
              for cg in range(2):
                  bank = 2 + (2 * i + cg) % 4
                  for k in range(8):
                      kb.op('pe', lambda en, k=k, cg=cg, bank=bank, p=p: en.matmul(ps[bank][:, :], lhsT=oT[p][:, k, :], rhs=Wo[:, k, cg * 512:(cg + 1) * 512],
                                                                                 start=(k == 0), stop=(k == 7)),
                            reads=[('C', p, 'oT'), ('Wo', k)], writes=[f'ps{bank}'])
                  kb.op('dve', lambda en, cg=cg, bank=bank, p=p: en.scalar_tensor_tensor(out=r1[p][:, cg * 512:(cg + 1) * 512], in0=hr[p][:, cg * 512:(cg + 1) * 512],
                                                                                       scalar=ALPHA, in1=ps[bank][:, :], op0=ALU.mult, op1=ALU.add),
                        reads=[f'ps{bank}', ('C', p, 'hr')], writes=[('C', p, 'x')])
              emit_layernorm(kb, ('C', p), r1[p][:, :], h1[p][:, :], g_t[:, :], b_t[:, :], eps_t, stt[p], mv[p], rs[p], gk='C_g', bk='C_b')
              kb.dma('sp', out=h1_d[rows, :], in_=h1[p][:, :], reads=[('C', p, 'o')], writes=[('h1d', i)])
          kb.barrier()

    with ExitStack() as st:
      if do2:
          W1 = kb.sbuf(st, "C_W1", [128, 8, 4096], BF16)
          W2 = kb.sbuf(st, "C_W2", [128, 32, 1024], BF16)
          Wg = kb.sbuf(st, "C_Wg", [128, 8, 1024], BF16)
          Wp = kb.sbuf(st, "C_Wp", [128, 2, 1024], BF16)
          ident = kb.sbuf(st, "C2_ident", [128, 128], BF16)
          eps_t = kb.sbuf(st, "C2_eps", [128, 1], F32)
          g_t = kb.sbuf(st, "C2_g", [128, 1024], F32)
          b_t = kb.sbuf(st, "C2_b", [128, 1024], F32)
          h1 = [kb.sbuf(st, f"C2_h1{i}", [128, 1024], F32) for i in range(2)]
          h1b = kb.sbuf(st, "C2_h1b", [128, 1024], BF16)
          pt = kb.sbuf(st, "C2_p", [128, 256], F32)
          pb = kb.sbuf(st, "C2_pb", [128, 256], BF16)
          h1T = kb.sbuf(st, "C2_h1T", [128, 8, 256], BF16)
          pT = kb.sbuf(st, "C2_pT", [128, 2, 256], BF16)
          hid = kb.sbuf(st, "C2_hid", [128, 32, 256], BF16)
          r32 = [kb.sbuf(st, f"C2_r32{i}", [128, 256], F32) for i in range(2)]
          sgm = kb.sbuf(st, "C2_sgm", [128, 1024], F32)
          r2 = kb.sbuf(st, "C2_r2", [128, 1024], F32)
          ho = kb.sbuf(st, "C2_ho", [128, 1024], F32)
          stt = kb.sbuf(st, "C2_st", [128, 2, 6], F32)
          mv = kb.sbuf(st, "C2_mv", [128, 2], F32)
          rs = kb.sbuf(st, "C2_rs", [128, 2], F32)
          ps = [kb.psum(st, f"C2_ps{i}", [128, 512]) for i in range(8)]

          kb.dma('pool', out=ident[:, :], in_=ident_d, writes=['ident'])
          kb.op('dve', lambda en: en.memset(eps_t[:, :], LN_EPS), writes=['eps'])
          kb.dma('sp', out=g_t[:, :], in_=ln2g_d.partition_broadcast(128), writes=['C2_g'])
          kb.dma('sp', out=b_t[:, :], in_=ln2b_d.partition_broadcast(128), writes=['C2_b'])
          for k in range(8):
              kb.dma('pool', out=W1[:, k, :], in_=w_ff1_d[k * 128:(k + 1) * 128, :], writes=[('W1', k)])
          for k in range(8):
              kb.dma('pool', out=Wg[:, k, :], in_=wg_d[k * 128:(k + 1) * 128, :], writes=[('Wg', k)])
          for k in range(2):
              kb.dma('pool', out=Wp[:, k, :], in_=wp_d[k * 128:(k + 1) * 128, :], writes=[('Wp', k)])
          for k in range(32):
              kb.dma('pool', out=W2[:, k, :], in_=w_ff2_d[k * 128:(k + 1) * 128, :], writes=[('W2', k)])

          for j in range(ntile // 2):
              for t in range(2):
                  i = 2 * j + t
                  rows = slice(i * 128, (i + 1) * 128)
                  kb.dma('sp', out=h1[t][:, :], in_=h1_d[rows, :], reads=[('h1d', i)], writes=[('C2', 'h1', t)])
                  kb.dma('sp', out=pt[:, :], in_=p_d[rows, :], writes=[('C2', 'p')])
                  kb.op('act', lambda en, t=t: en.activation(out=h1b[:, :], in_=h1[t][:, :], func=AF.Copy), reads=[('C2', 'h1', t)], writes=[('C2', 'h1b')])
                  kb.op('dve', lambda en: en.tensor_copy(out=pb[:, :], in_=pt[:, :]), reads=[('C2', 'p')], writes=[('C2', 'pb')])
                  for k in range(8):
                      b = 4 + k // 4
                      kb.op('pe', lambda en, k=k, b=b: en.matmul(ps[b][:, (k % 4) * 128:(k % 4 + 1) * 128], lhsT=h1b[:, k * 128:(k + 1) * 128], rhs=ident[:, :], start=True, stop=True),
                            reads=[('C2', 'h1b'), 'ident'], writes=[f'ps{b}'])
                  for k in range(2):
                      kb.op('pe', lambda en, k=k: en.matmul(ps[6][:, k * 128:(k + 1) * 128], lhsT=pb[:, k * 128:(k + 1) * 128], rhs=ident[:, :], start=True, stop=True),
                            reads=[('C2', 'pb'), 'ident'], writes=['ps6'])
                  for b in range(2):
                      kb.op('dve' if b == 0 else 'act', (lambda en, b=b, t=t: en.tensor_copy(out=h1T[:, 4 * b:4 * b + 4, t * 128:(t + 1) * 128], in_=ps[4 + b][:, :].rearrange("p (a t) -> p a t", t=128)))
                            if b == 0 else (lambda en, b=b, t=t: en.activation(out=h1T[:, 4 * b:4 * b + 4, t * 128:(t + 1) * 128], in_=ps[4 + b][:, :].rearrange("p (a t) -> p a t", t=128), func=AF.Copy)),
                            reads=[f'ps{4 + b}'], writes=[('C2', 'h1T')])
                  kb.op('dve', lambda en, t=t: en.tensor_copy(out=pT[:, 0:2, t * 128:(t + 1) * 128], in_=ps[6][:, 0:256].rearrange("p (a t) -> p a t", t=128)),
                        reads=['ps6'], writes=[('C2', 'pT')])
              STG = 9
              for fc in range(32 if STG >= 2 else 0):
                  bank, off = fc % 4, 0
                  pk = f'ps{bank}'
                  for k in range(8):
                      kb.op('pe', lambda en, k=k, fc=fc, bank=bank, off=off: en.matmul(ps[bank][:, off:off + 256], lhsT=W1[:, k, fc * 128:(fc + 1) * 128], rhs=h1T[:, k, :],
                                                                                     start=(k == 0), stop=(k == 7)),
                            reads=[('C2', 'h1T'), ('W1', k)], writes=[pk])
                  q = fc % 2
                  kb.op('act', lambda en, bank=bank, off=off, q=q: en.activation(out=r32[q][:, :], in_=ps[bank][:, off:off + 256], func=AF.Relu),
                        reads=[pk], writes=[('C2', 'r32', q)])
                  kb.op('pool', lambda en, fc=fc, q=q: en.tensor_tensor(out=hid[:, fc, :], in0=r32[q][:, :], in1=r32[q][:, :], op=ALU.mult),
                        reads=[('C2', 'r32', q)], writes=[('C2', 'hid', fc)])
              for t in range(2 if STG >= 3 else 0):
                  i = 2 * j + t
                  rows = slice(i * 128, (i + 1) * 128)
                  tc_ = slice(t * 128, (t + 1) * 128)
                  for cg in range(2):
                      cs = slice(cg * 512, (cg + 1) * 512)
                      for k in range(8):
                          kb.op('pe', lambda en, k=k, cs=cs, tc_=tc_: en.matmul(ps[4][:, :], lhsT=h1T[:, k, tc_], rhs=Wg[:, k, cs], start=(k == 0), stop=(k == 7)),
                                reads=[('C2', 'h1T'), ('Wg', k)], writes=['ps4'])
                      kb.op('act', lambda en, cs=cs: en.activation(out=sgm[:, cs], in_=ps[4][:, :], func=AF.Sigmoid), reads=['ps4'], writes=[('C2', 'sgm', cg)])
                      for k in range(2):
                          kb.op('pe', lambda en, k=k, cs=cs, tc_=tc_: en.matmul(ps[5][:, :], lhsT=pT[:, k, tc_], rhs=Wp[:, k, cs], start=(k == 0), stop=(k == 1)),
                                reads=[('C2', 'pT'), ('Wp', k)], writes=['ps5'])
                      kb.op('dve', lambda en, cs=cs: en.tensor_tensor(out=sgm[:, cs], in0=ps[5][:, :], in1=sgm[:, cs], op=ALU.mult),
                            reads=['ps5', ('C2', 'sgm', cg)], writes=[('C2', 'sgm', cg)])
                      bank = 6 + cg
                      for fc in range(32):
                          kb.op('pe', lambda en, fc=fc, cs=cs, tc_=tc_, bank=bank: en.matmul(ps[bank][:, :], lhsT=hid[:, fc, tc_], rhs=W2[:, fc, cs], start=(fc == 0), stop=(fc == 31)),
                                reads=[('C2', 'hid', fc), ('W2', fc)], writes=[f'ps{bank}'])
                      kb.op('dve', lambda en, cs=cs, bank=bank, t=t: en.scalar_tensor_tensor(out=r2[:, cs], in0=h1[t][:, cs], scalar=ALPHA, in1=ps[bank][:, :], op0=ALU.mult, op1=ALU.add),
                            reads=[f'ps{bank}', ('C2', 'h1', t)], writes=[('C2', 'r2', cg)])
                      kb.op('pool', lambda en, cs=cs: en.tensor_tensor(out=r2[:, cs], in0=r2[:, cs], in1=sgm[:, cs], op=ALU.add),
                            reads=[('C2', 'r2', cg), ('C2', 'sgm', cg)], writes=[('C2', 'r2', cg)])
                  kb.op('pool', lambda en: en.tensor_copy(out=r2[:, 0:1], in_=r2[:, 0:1]), reads=[('C2', 'r2', 0), ('C2', 'r2', 1)], writes=[('C2', 'x')])
                  emit_layernorm(kb, ('C2',), r2[:, :], ho[:, :], g_t[:, :], b_t[:, :], eps_t, stt, mv, rs, gk='C2_g', bk='C2_b')
                  kb.dma('sp', out=hout_d[rows, :], in_=ho[:, :], reads=[('C2', 'o')], writes=[('hout', i)])
          kb.barrier()


def build_C(ntile=NT, do1=True, do2=True):
    kb = KB()
    nc = kb.nc
    def ein(name, shape, dt=F32):
        return nc.dram_tensor(name, list(shape), dt, kind="ExternalInput").ap()
    O = ein("O", [TOK, D], BF16)
    hres = ein("hres", [TOK, D])
    p = ein("p", [TOK, 256])
    w_out = ein("w_out", [D, D]); ssdg = ein("ssdg", [1, 512]); ln1g = ein("ln1g", [1, D]); ln1b = ein("ln1b", [1, D])
    w_ff1 = ein("w_ff1", [D, 4096]); w_ff2 = ein("w_ff2", [4096, D]); wg = ein("wg", [D, D]); wp = ein("wp", [256, D])
    ln2g = ein("ln2g", [1, D]); ln2b = ein("ln2b", [1, D]); ident = ein("ident", [128, 128])
    h1 = nc.dram_tensor("h1", [TOK, D], F32, kind=("Internal" if do2 else "ExternalOutput") if do1 else "ExternalInput").ap()
    hout = nc.dram_tensor("hout", [TOK, D], F32, kind="ExternalOutput").ap()
    phase_C(kb, O, hres, h1, hout, p, w_out, ssdg, ln1g, ln1b, w_ff1, w_ff2, wg, wp, ln2g, ln2b, ident, ntile, do1, do2)
    kb.finish()
    return kb


O_ROWS = np.concatenate([np.arange(0, 128), 256 + np.arange(0, 128), 512 + np.arange(0, 256),
                         128 + np.arange(0, 128), 256 + 128 + np.arange(0, 128), 512 + 256 + np.arange(0, 256)])


def phase_B(kb, nch, lamc_d, load_F, Fdt_d, OB_d, consts_d, poolA_d, poolw_d, pscale_d, lamv_d, gdiff_d, convw_d, convb_d, ssdv_d):
    SCL = 32 ** -0.5
    with ExitStack() as st:
        sb = lambda name, shape, dt=F32: kb.sbuf(st, "B_" + name, shape, dt)
        cst = sb("cst", [128, 4, 128])
        identb = sb("identb", [128, 128], BF16)
        trib = sb("trib", [128, 128], BF16)
        negm4 = sb("negm4", [128, 4, 128])
        pA = sb("pA", [128, 6, 128])
        pw32 = sb("pw32", [64, 2, 64]); pwb = sb("pwb", [64, 2, 64], BF16); psc = sb("psc", [64, 128])
        lamt = sb("lamt", [128, 128]); lamw = sb("lamw", [128, 8]); lc = sb("lc", [128, 2])
        gd = sb("gd", [128, 64])
        cw = sb("cw", [128, 4, 4]); cb = sb("cb", [128, 4])
        diagW = sb("diagW", [128, 16, 128], BF16)
        sv = sb("sv", [128, 12]); A_t = sb("A_t", [128, 4]); D_t = sb("D_t", [128, 4, 64])
        eps_t = sb("eps", [128, 1]); one_t = sb("one", [128, 1])
        QK = sb("QK", [64, 2, 2, nch * 128], BF16)
        Va = sb("Va", [128, nch, 2, 65], BF16)
        Fc = [sb(f"Fc{i}", [128, 1280], BF16) for i in range(2)]
        u32 = [sb(f"u32{i}", [128, 128]) for i in range(2)]
        plT = sb("plT", [64, 2, 128], BF16)
        ob = [sb(f"ob{i}", [128, 512], BF16) for i in range(2)]
        xin = [sb(f"xin{i}", [128, 4, 131], BF16) for i in range(2)]
        sT = sb("sT", [128, 4, 128], BF16)
        dtr = sb("dtr", [128, 8]); dtv = sb("dtv", [128, 4]); dtA = sb("dtA", [128, 4]); dtd = sb("dtd", [128, 4])
        acs = sb("acs", [128, 8]); t12 = sb("t12", [128, 12]); e12 = sb("e12", [128, 12]); nacs = sb("nacs", [128, 4])
        rhsS = sb("rhsS", [128, 4, 128])
        LT = sb("LT", [128, 4, 128]); MT = sb("MT", [128, 4, 128], BF16)
        xdt = sb("xdt", [128, 4, 64], BF16); xdd = sb("xdd", [128, 4, 64], BF16); xsD = sb("xsD", [128, 256])
        Btok = sb("Btok", [128, 128], BF16)
        prev32 = sb("prev32", [128, 4, 64]); prevb = sb("prevb", [128, 256], BF16)
        yo = sb("yo", [128, 256]); y1 = sb("y1", [128, 256]); sz = sb("sz", [128, 256])
        Eb = [sb(f"E{i}", [128, 512], BF16) for i in range(6)]
        accS = [sb(f"accS{i}", [65, 512]) for i in range(4)]
        od = sb("od", [128, 2, 64]); t64 = sb("t64", [128, 2, 64]); rr = sb("rr", [128, 8]); ssd2 = sb("ssd2", [128, 4])
        junk = sb("junk", [128, 64]); obd = [sb(f"obd{i}", [128, 128], BF16) for i in range(2)]
        ps = [kb.psum(st, f"B_ps{i}", [128, 512]) for i in range(8)]

        kb.dma('sp', out=cst[:, :, :], in_=consts_d.rearrange("a p f -> p a f"), writes=['cst'])
        kb.dma('sp', out=pA[:, :, :], in_=poolA_d.rearrange("a g p f -> p (a g) f"), writes=['pA'])
        kb.dma('sp', out=pw32[:, :, :], in_=poolw_d.rearrange("g c d -> c g d"), writes=['pw32'])
        kb.dma('sp', out=psc[:, :], in_=pscale_d.partition_broadcast(64), writes=['psc'])
        kb.dma('sp', out=lamt[:, :], in_=lamv_d.partition_broadcast(128), writes=['lamt'])
        kb.dma('sp', out=gd[:, :], in_=gdiff_d.partition_broadcast(128), writes=['gd'])
        kb.dma('sp', out=lc[:, :], in_=lamc_d.partition_broadcast(128), writes=['lc'])
        kb.dma('sp', out=cw[:, :, :], in_=convw_d.rearrange("p (c k) -> p c k", k=4), writes=['cw'])
        kb.dma('sp', out=cb[:, :], in_=convb_d, writes=['cb'])
        kb.dma('sp', out=sv[:, :], in_=ssdv_d.partition_broadcast(128), writes=['sv'])
        kb.op('dve', lambda en: en.memset(eps_t[:, :], LN_EPS), writes=['eps'])
        kb.op('dve', lambda en: en.memset(one_t[:, :], 1.0), writes=['one'])
        kb.op('dve', lambda en: en.tensor_copy(out=identb[:, :], in_=cst[:, 0, :]), reads=['cst'], writes=['identb'])
        kb.op('dve', lambda en: en.tensor_copy(out=trib[:, :], in_=cst[:, 1, :]), reads=['cst'], writes=['trib'])
        kb.op('dve', lambda en: en.tensor_copy(out=negm4[:, :, :], in_=cst[:, 2, :].unsqueeze(1).broadcast_to([128, 4, 128])), reads=['cst'], writes=['negm4'])
        kb.op('dve', lambda en: en.tensor_tensor(out=pwb[:, :, :], in0=pw32[:, :, :], in1=psc[:, :].rearrange("p (g d) -> p g d", d=64), op=ALU.mult),
              reads=['pw32', 'psc'], writes=['pwb'])
        kb.op('dve', lambda en: en.tensor_tensor(out=lamt[:, 0:32], in0=lamt[:, 0:32], in1=lamt[:, 32:64], op=ALU.mult), reads=['lamt'], writes=['lamt'])
        kb.op('dve', lambda en: en.tensor_tensor(out=lamt[:, 64:96], in0=lamt[:, 64:96], in1=lamt[:, 96:128], op=ALU.mult), reads=['lamt'], writes=['lamt'])
        kb.op('dve', lambda en: en.tensor_reduce(out=lamw[:, 0:1], in_=lamt[:, 0:32], axis=AX.X, op=ALU.add), reads=['lamt'], writes=['lamw'])
        kb.op('dve', lambda en: en.tensor_reduce(out=lamw[:, 1:2], in_=lamt[:, 64:96], axis=AX.X, op=ALU.add), reads=['lamw', 'lamt'], writes=['lamw'])
        kb.op('act', lambda en: en.activation(out=lamw[:, 2:4], in_=lamw[:, 0:2], func=AF.Exp), reads=['lamw'], writes=['lamw'])
        kb.op('dve', lambda en: en.tensor_tensor(out=lamw[:, 4:5], in0=lamw[:, 3:4], in1=lamw[:, 2:3], op=ALU.subtract), reads=['lamw'], writes=['lamw'])
        kb.op('dve', lambda en: en.tensor_tensor(out=lamw[:, 4:5], in0=lamw[:, 4:5], in1=lc[:, 0:1], op=ALU.subtract), reads=['lamw', 'lc'], writes=['lamw'])
        kb.op('dve', lambda en: en.tensor_scalar(out=gd[:, :], in0=gd[:, :], scalar1=lc[:, 1:2], scalar2=None, op0=ALU.mult), reads=['gd', 'lc'], writes=['gd'])
        for j in range(4):
            for k in range(4):
                kb.op('dve', lambda en, j=j, k=k: en.tensor_scalar(out=diagW[:, j * 4 + k, :], in0=cst[:, 0, :], scalar1=cw[:, j, k:k + 1], scalar2=None, op0=ALU.mult),
                      reads=['cst', 'cw'], writes=['diagW'])
        kb.op('act', lambda en: en.activation(out=A_t[:, :], in_=sv[:, 4:8], func=AF.Exp), reads=['sv'], writes=['A_t'])
        kb.op('dve', lambda en: en.tensor_scalar(out=A_t[:, :], in0=A_t[:, :], scalar1=-1.0, scalar2=None, op0=ALU.mult), reads=['A_t'], writes=['A_t'])
        kb.op('dve', lambda en: en.tensor_copy(out=D_t[:, :, :], in_=sv[:, 8:12].unsqueeze(2).broadcast_to([128, 4, 64])), reads=['sv'], writes=['D_t'])
        kb.op('pool', lambda en: en.memset(Va[:, :, :, 64:65], 1.0), writes=['Va1'])
        kb.op('dve', lambda en: en.memset(prev32[:, :, :], 0.0), writes=['prev32'])
        kb.op('pool', lambda en: en.memset(prevb[:, :], 0.0), writes=['prevb'])
        kb.op('pool', lambda en: en.memset(xin[0][:, :, 0:3], 0.0), writes=[('xin', 0, 'h')])

        for c in range(nch):
            p = c % 2
            q = 1 - p
            rows = slice(c * 128, (c + 1) * 128)
            Fk = ('Fc', p)
            load_F(kb, c, Fc[p], Fk)
            kb.dma('sp', out=dtr[:, :], in_=Fdt_d[rows, :], writes=['dtr'])
            kb.op('pool', lambda en, p=p: en.tensor_copy(out=u32[p][:, :], in_=Fc[p][:, 0:128]), reads=[Fk], writes=[('u32', p)])
            for g in range(2):
                a_idx = (4 + g) if c == 0 else g
                kb.op('pe', lambda en, g=g, a_idx=a_idx, p=p: en.matmul(ps[0][0:64, g * 128:(g + 1) * 128], lhsT=u32[p][:, g * 64:(g + 1) * 64], rhs=pA[:, a_idx, :],
                                                                   start=True, stop=(c == 0)),
                      reads=[('u32', p), 'pA'], writes=['psb0'])
                if c > 0:
                    kb.op('pe', lambda en, g=g, q=q: en.matmul(ps[0][0:64, g * 128:(g + 1) * 128], lhsT=u32[q][:, g * 64:(g + 1) * 64], rhs=pA[:, 2 + g, :],
                                                             start=False, stop=True),
                          reads=[('u32', q), 'pA'], writes=['psb0'])
            kb.op('act', lambda en: en.activation(out=plT[:, :, :], in_=ps[0][0:64, 0:256].rearrange("p (g t) -> p g t", t=128), func=AF.Copy),
                  reads=['psb0'], writes=['plT'])
            for g in range(2):
                kb.op('pe', lambda en, g=g: en.matmul(ps[0][:, 256 + g * 64:256 + (g + 1) * 64], lhsT=plT[:, g, :], rhs=pwb[:, g, :], start=True, stop=True),
                      reads=['plT', 'pwb'], writes=['psb0'])
            kb.op('act', lambda en, p=p: en.activation(out=ob[p][:, 0:128], in_=ps[0][:, 256:384], func=AF.Copy), reads=['psb0'], writes=[('ob', p, 0)])
            kb.dma('sp', out=OB_d[rows, 0:128], in_=ob[p][:, 0:128], reads=[('ob', p, 0)])
            for a in range(2):
                for h in range(2):
                    kb.op('pe', lambda en, a=a, h=h, p=p: en.matmul(ps[1][0:64, (2 * a + h) * 128:(2 * a + h + 1) * 128], lhsT=Fc[p][:, 128 + a * 128 + h * 64:192 + a * 128 + h * 64],
                                                                  rhs=identb[:, :], start=True, stop=True),
                          reads=[Fk, 'identb'], writes=['psb1'])
            kb.op('dve', lambda en, c=c: en.tensor_copy(out=QK[:, :, :, c * 128:(c + 1) * 128], in_=ps[1][0:64, :].rearrange("p (a h t) -> p a h t", h=2, t=128)),
                  reads=['psb1'], writes=[('QK', c)])
            kb.op('pool', lambda en, c=c, p=p: en.tensor_copy(out=Va[:, c, :, 0:64], in_=Fc[p][:, 384:512].rearrange("p (h v) -> p h v", v=64)),
                  reads=[Fk, 'Va1'], writes=[('Va', c)])
            for j in range(4):
                kb.op('pe', lambda en, j=j, p=p: en.matmul(ps[2][:, j * 128:(j + 1) * 128], lhsT=Fc[p][:, 768 + j * 128:896 + j * 128], rhs=identb[:, :], start=True, stop=True),
                      reads=[Fk, 'identb'], writes=['psb2'])
            if c > 0:
                kb.op('pool', lambda en, p=p, q=q: en.tensor_copy(out=xin[p][:, :, 0:3], in_=xin[q][:, :, 128:131]), reads=[('xin', q)], writes=[('xin', p, 'h')])
            kb.op('dve', lambda en, p=p: en.tensor_copy(out=xin[p][:, :, 3:131], in_=ps[2][:, :].rearrange("p (j t) -> p j t", t=128)),
                  reads=['psb2'], writes=[('xin', p)])
            for j in range(4):
                for k in range(4):
                    kb.op('pe', lambda en, j=j, k=k, p=p: en.matmul(ps[3 - j % 2][:, j * 128:(j + 1) * 128], lhsT=diagW[:, j * 4 + k, :], rhs=xin[p][:, j, k:k + 128],
                                                                  start=(k == 0), stop=(k == 3)),
                          reads=[('xin', p), ('xin', p, 'h'), 'diagW'], writes=[f'psb{3 - j % 2}'])
                kb.op('act', lambda en, j=j: en.activation(out=sT[:, j, :], in_=ps[3 - j % 2][:, j * 128:(j + 1) * 128], func=AF.Silu, bias=cb[:, j:j + 1], scale=1.0),
                      reads=[f'psb{3 - j % 2}', 'cb'], writes=[('sT', j)])
            for j in range(3):
                kb.op('pe', lambda en, j=j: en.matmul(ps[4][:, j * 128:(j + 1) * 128], lhsT=sT[:, j, :], rhs=identb[:, :], start=True, stop=True),
                      reads=[('sT', j), 'identb'], writes=['psb4'])
            kb.op('dve', lambda en: en.tensor_tensor(out=dtv[:, :], in0=dtr[:, 0:4], in1=sv[:, 0:4], op=ALU.add), reads=['dtr', 'sv'], writes=['dtv'])
            kb.op('act', lambda en: en.activation(out=dtv[:, :], in_=dtv[:, :], func=AF.Exp), reads=['dtv'], writes=['dtv'])
            kb.op('act', lambda en: en.activation(out=dtv[:, :], in_=dtv[:, :], func=AF.Ln, bias=one_t[:, 0:1], scale=1.0), reads=['dtv', 'one'], writes=['dtv'])
            kb.op('dve', lambda en: en.tensor_tensor(out=dtA[:, :], in0=dtv[:, :], in1=A_t[:, :], op=ALU.mult), reads=['dtv', 'A_t'], writes=['dtA'])
            kb.op('pe', lambda en: en.matmul(ps[7][:, 256:260], lhsT=cst[:, 1, :], rhs=dtA[:, :], start=True, stop=True), reads=['cst', 'dtA'], writes=['psb7'])
            kb.op('pe', lambda en: en.matmul(ps[7][:, 260:264], lhsT=cst[:, 3, :], rhs=dtA[:, :], start=True, stop=True), reads=['cst', 'dtA'], writes=['psb7'])
            kb.op('dve', lambda en: en.tensor_copy(out=acs[:, :], in_=ps[7][:, 256:264]), reads=['psb7'], writes=['acs'])
            kb.op('dve', lambda en: en.tensor_copy(out=t12[:, 0:4], in_=acs[:, 0:4]), reads=['acs'], writes=['t12a'])
            kb.op('dve', lambda en: en.tensor_tensor(out=t12[:, 4:8], in0=acs[:, 4:8], in1=acs[:, 0:4], op=ALU.subtract), reads=['acs'], writes=['t12b'])
            kb.op('dve', lambda en: en.tensor_copy(out=t12[:, 8:12], in_=acs[:, 4:8]), reads=['acs'], writes=['t12c'])
            kb.op('act', lambda en: en.activation(out=e12[:, :], in_=t12[:, :], func=AF.Exp), reads=['t12a', 't12b', 't12c'], writes=['e12'])
            kb.op('dve', lambda en: en.tensor_scalar(out=nacs[:, :], in0=acs[:, 0:4], scalar1=-1.0, scalar2=None, op0=ALU.mult), reads=['acs'], writes=['nacs'])
            kb.op('dve', lambda en: en.tensor_tensor(out=dtd[:, :], in0=dtv[:, :], in1=e12[:, 4:8], op=ALU.mult), reads=['dtv', 'e12'], writes=['dtd'])
            kb.op('dve', lambda en: en.tensor_tensor(out=rhsS[:, :, :], in0=dtA[:, :].unsqueeze(2).broadcast_to([128, 4, 128]),
                                                     in1=cst[:, 1, :].unsqueeze(1).broadcast_to([128, 4, 128]), op=ALU.mult),
                  reads=['dtA', 'cst'], writes=['rhsS'])
            kb.op('pe', lambda en: en.matmul(ps[5][:, :], lhsT=cst[:, 3, :], rhs=rhsS[:, :, :].rearrange("p h l -> p (h l)"), start=True, stop=False),
                  reads=['cst', 'rhsS'], writes=['psb5'])
            kb.op('pe', lambda en: en.matmul(ps[5][:, :], lhsT=cst[:, 0, :], rhs=negm4[:, :, :].rearrange("p h l -> p (h l)"), start=False, stop=True),
                  reads=['cst', 'negm4'], writes=['psb5'])
            for h in range(4):
                kb.op('act', lambda en, h=h: en.activation(out=LT[:, h, :], in_=ps[5][:, h * 128:(h + 1) * 128], func=AF.Exp, bias=nacs[:, h:h + 1], scale=1.0),
                      reads=['psb5', 'nacs'], writes=[('LT', h)])
            kb.op('pe', lambda en: en.matmul(ps[0][:, 384:512], lhsT=sT[:, 2, :], rhs=sT[:, 3, :], start=True, stop=True), reads=[('sT', 2), ('sT', 3)], writes=['psb0'])
            kb.op('dve', lambda en: en.tensor_tensor(out=MT[:, :, :], in0=LT[:, :, :], in1=ps[0][:, 384:512].unsqueeze(1).broadcast_to([128, 4, 128]), op=ALU.mult),
                  reads=[('LT', h) for h in range(4)] + ['psb0'], writes=['MT'])
            xs_ps = ps[4][:, 0:256].rearrange("p (h v) -> p h v", v=64)
            kb.op('dve', lambda en: en.tensor_tensor(out=xdt[:, :, :], in0=xs_ps, in1=dtv[:, :].unsqueeze(2).broadcast_to([128, 4, 64]), op=ALU.mult),
                  reads=['psb4', 'dtv'], writes=['xdt'])
            kb.op('dve', lambda en: en.tensor_tensor(out=xdd[:, :, :], in0=xs_ps, in1=dtd[:, :].unsqueeze(2).broadcast_to([128, 4, 64]), op=ALU.mult),
                  reads=['psb4', 'dtd'], writes=['xdd'])
            kb.op('dve', lambda en: en.tensor_tensor(out=xsD[:, :], in0=ps[4][:, 0:256], in1=D_t[:, :, :].rearrange("p h v -> p (h v)"), op=ALU.mult),
                  reads=['psb4', 'D_t'], writes=['xsD'])
            kb.op('act', lambda en: en.activation(out=Btok[:, :], in_=ps[4][:, 256:384], func=AF.Copy), reads=['psb4'], writes=['Btok'])
            for h in range(4):
                kb.op('pe', lambda en, h=h: en.matmul(ps[6][:, h * 64:(h + 1) * 64], lhsT=MT[:, h, :], rhs=xdt[:, h, :], start=True, stop=True),
                      reads=['MT', 'xdt'], writes=['psb6'])
            kb.op('pe', lambda en: en.matmul(ps[6][:, 256:512], lhsT=sT[:, 3, :], rhs=prevb[:, :], start=True, stop=True), reads=[('sT', 3), 'prevb'], writes=['psb6'])
            kb.op('dve', lambda en: en.tensor_tensor(out=yo[:, :].rearrange("p (h v) -> p h v", v=64), in0=ps[6][:, 256:512].rearrange("p (h v) -> p h v", v=64),
                                                     in1=e12[:, 0:4].unsqueeze(2).broadcast_to([128, 4, 64]), op=ALU.mult),
                  reads=['psb6', 'e12'], writes=['yo'])
            kb.op('dve', lambda en: en.tensor_tensor(out=y1[:, :], in0=ps[6][:, 0:256], in1=yo[:, :], op=ALU.add), reads=['psb6', 'yo'], writes=['y1'])
            kb.op('pool', lambda en: en.tensor_tensor(out=y1[:, :], in0=y1[:, :], in1=xsD[:, :], op=ALU.add), reads=['y1', 'xsD'], writes=['y1'])
            kb.op('pe', lambda en: en.matmul(ps[7][:, 0:256], lhsT=Btok[:, :], rhs=xdd[:, :, :].rearrange("p h v -> p (h v)"), start=True, stop=True),
                  reads=['Btok', 'xdd'], writes=['psb7'])
            kb.op('dve', lambda en: en.tensor_tensor(out=prev32[:, :, :], in0=prev32[:, :, :], in1=e12[:, 8:12].unsqueeze(2).broadcast_to([128, 4, 64]), op=ALU.mult),
                  reads=['prev32', 'e12'], writes=['prev32'])
            kb.op('dve', lambda en: en.tensor_tensor(out=prev32[:, :, :], in0=prev32[:, :, :], in1=ps[7][:, 0:256].rearrange("p (h v) -> p h v", v=64), op=ALU.add),
                  reads=['prev32', 'psb7'], writes=['prev32'])
            kb.op('pool', lambda en: en.tensor_copy(out=prevb[:, :], in_=prev32[:, :, :].rearrange("p h v -> p (h v)")), reads=['prev32'], writes=['prevb'])
            kb.op('act', lambda en, p=p: en.activation(out=sz[:, :], in_=Fc[p][:, 512:768], func=AF.Silu), reads=[Fk], writes=['sz'])
            kb.op('pool', lambda en, p=p: en.tensor_tensor(out=ob[p][:, 256:512], in0=y1[:, :], in1=sz[:, :], op=ALU.mult), reads=['y1', 'sz'], writes=[('ob', p, 1)])
            kb.dma('sp', out=OB_d[rows, 256:512], in_=ob[p][:, 256:512], reads=[('ob', p, 1)])

        ngrp = nch // 4
        ecnt = 0
        for Qg in range(ngrp):
            nk = 4 * Qg + 4
            for kc in range(nk):
                i = kc - 4 * Qg
                off = 0 if i < 0 else i * 128
                for hm in range(4):
                    h = hm // 2
                    rs_ = slice(32 * (hm % 2), 32 * (hm % 2) + 32)
                    kb.op('pe', lambda en, hm=hm, h=h, rs_=rs_, kc=kc, off=off, Qg=Qg: en.matmul(ps[hm][:, off:512], lhsT=QK[rs_, 1, h, kc * 128:(kc + 1) * 128],
                                                                                              rhs=QK[rs_, 0, h, Qg * 512 + off:Qg * 512 + 512], start=True, stop=True),
                          reads=[('QK', kc)] + [('QK', 4 * Qg + a) for a in range(4)], writes=[f'psb{hm}'])
                    e = Eb[ecnt % 6]
                    ek = ('E', ecnt % 6)
                    ecnt += 1
                    kb.op('act', lambda en, hm=hm, off=off, e=e: en.activation(out=e[:, off:512], in_=ps[hm][:, off:512], func=AF.Exp, scale=SCL),
                          reads=[f'psb{hm}'], writes=[ek])
                    if i >= 0:
                        kb.op('pool', lambda en, off=off, e=e: en.tensor_tensor(out=e[:, off:off + 128], in0=e[:, off:off + 128], in1=trib[:, :], op=ALU.mult),
                              reads=[ek, 'trib'], writes=[ek])
                    kb.op('pe', lambda en, hm=hm, h=h, kc=kc, off=off, e=e, nk=nk: en.matmul(ps[4 + hm][0:65, off:512], lhsT=Va[:, kc, h, :], rhs=e[:, off:512],
                                                                                          start=(kc == 0), stop=(kc == nk - 1)),
                          reads=[ek, ('Va', kc), 'Va1'], writes=[f'psb{4 + hm}'])
            for hm in range(4):
                if hm % 2 == 0:
                    kb.op('dve', lambda en, hm=hm: en.tensor_copy(out=accS[hm][:, :], in_=ps[4 + hm][0:65, :]), reads=[f'psb{4 + hm}'], writes=[('accS', hm)])
                else:
                    kb.op('act', lambda en, hm=hm: en.activation(out=accS[hm][:, :], in_=ps[4 + hm][0:65, :], func=AF.Copy), reads=[f'psb{4 + hm}'], writes=[('accS', hm)])
            for i in range(4):
                c = 4 * Qg + i
                p = c % 2
                rows = slice(c * 128, (c + 1) * 128)
                for hm in range(4):
                    kb.op('pe', lambda en, hm=hm, i=i: en.matmul(ps[0][:, hm * 65:(hm + 1) * 65], lhsT=accS[hm][:, i * 128:(i + 1) * 128], rhs=cst[0:65, 0, 0:65], start=True, stop=True),
                          reads=[('accS', hm), 'cst'] + ['psb0'], writes=['psb0'])
                pv = ps[0][:, 0:260].rearrange("p (a b) -> p a b", b=65)
                kb.op('dve', lambda en: en.reciprocal(out=rr[:, 0:4], in_=pv[:, :, 64]), reads=['psb0'], writes=['rr'])
                kb.op('dve', lambda en: en.tensor_scalar(out=rr[:, 4:6], in0=rr[:, 0:4].rearrange("p (h j) -> p h j", j=2)[:, :, 1], scalar1=lamw[:, 4:5], scalar2=None, op0=ALU.mult),
                      reads=['rr', 'lamw'], writes=['rr2'])
                for h in range(2):
                    kb.op('dve', lambda en, h=h: en.tensor_scalar(out=t64[:, h, :], in0=pv[:, 2 * h, 0:64], scalar1=rr[:, 2 * h:2 * h + 1], scalar2=None, op0=ALU.mult),
                          reads=['psb0', 'rr'], writes=[('t64', h)])
                    kb.op('dve', lambda en, h=h: en.scalar_tensor_tensor(out=od[:, h, :], in0=pv[:, 2 * h + 1, 0:64], scalar=rr[:, 4 + h:5 + h], in1=t64[:, h, :], op0=ALU.mult, op1=ALU.add),
                          reads=['psb0', 'rr2', ('t64', h)], writes=[('od', h)])
                    kb.op('act', lambda en, h=h: en.activation(out=junk[:, :], in_=od[:, h, :], func=AF.Square, accum_out=ssd2[:, h:h + 1]),
                          reads=[('od', h)], writes=['junk', ('ssd2', h)])
                kb.op('act', lambda en: en.activation(out=ssd2[:, 2:4], in_=ssd2[:, 0:2], func=AF.Sqrt, bias=eps_t[:, 0:1], scale=1.0 / 64.0),
                      reads=[('ssd2', 0), ('ssd2', 1), 'eps'], writes=['ssd2b'])
                kb.op('dve', lambda en: en.reciprocal(out=ssd2[:, 0:2], in_=ssd2[:, 2:4]), reads=['ssd2b'], writes=[('ssd2', 0), ('ssd2', 1)])
                for h in range(2):
                    kb.op('dve', lambda en, h=h, p=p: en.scalar_tensor_tensor(out=obd[p][:, h * 64:(h + 1) * 64], in0=od[:, h, :], scalar=ssd2[:, h:h + 1], in1=gd[:, :], op0=ALU.mult, op1=ALU.mult),
                          reads=[('od', h), ('ssd2', h), 'gd'], writes=[('obd', p)])
                kb.dma('sp', out=OB_d[rows, 128:256], in_=obd[p][:, :], reads=[('obd', p)])
        kb.barrier()


POOL_WINDOWS = (2, 4, 8, 16)


def pool_matrices(r):
    A = np.zeros((3, 2, 128, 128), np.float32)
    for g in range(2):
        w = POOL_WINDOWS[2 * r + g]
        for t in range(128):
            for s in range(max(0, t - w + 1), t + 1):
                A[0, g, s, t] = 1.0 / w
                A[2, g, s, t] = 1.0 / min(t + 1, w)
            A[0, g, t, t] -= 1.0
            A[2, g, t, t] -= 1.0
            for sp in range(128 + t - w + 1, 128):
                if sp >= 0:
                    A[1, g, sp, t] = 1.0 / w
    return A


def const_mats():
    idx = np.arange(128)
    tri = (idx[:, None] <= idx[None, :]).astype(np.float32)
    return np.stack([np.eye(128, dtype=np.float32), tri, (tri - 1.0) * 30000.0, np.ones((128, 128), np.float32)])


def build_B(nch, layer=0):
    kb = KB()
    nc = kb.nc
    def ein(name, shape, dt=F32):
        return nc.dram_tensor(name, list(shape), dt, kind="ExternalInput").ap()
    F = ein("F", [nch * 128, HW], BF16)
    Fdt = ein("Fdt", [nch * 128, 8])
    consts = ein("consts", [4, 128, 128]); poolA = ein("poolA", [3, 2, 128, 128]); poolw = ein("poolw", [2, 64, 64]); pscale = ein("pscale", [1, 128])
    lamc = ein("lamc", [1, 2]); lamv = ein("lamv", [1, 128]); gdiff = ein("gdiff", [1, 64]); convw = ein("convw", [128, 16]); convb = ein("convb", [128, 4]); ssdv = ein("ssdv", [1, 12])
    OB = nc.dram_tensor("OB", [nch * 128, 512], BF16, kind="ExternalOutput").ap()
    def load_F(kb, c, tile, key):
        kb.dma('sp', out=tile[:, :], in_=F[c * 128:(c + 1) * 128, :], writes=[key])
    phase_B(kb, nch, lamc, load_F, Fdt, OB, consts, poolA, poolw, pscale, lamv, gdiff, convw, convb, ssdv)
    kb.finish()
    return kb


def mixer_params(inp, l, r):
    xs = slice(256 * r, 256 * r + 256)
    ch = np.concatenate([np.arange(256 * r, 256 * r + 256), 512 + np.arange(128 * r, 128 * r + 128), 768 + np.arange(128 * r, 128 * r + 128)])
    h4 = slice(4 * r, 4 * r + 4)
    lambda_init = 0.8 - 0.6 * math.exp(-0.3 * l)
    return {
        'lamc': np.array([[lambda_init, 1.0 - lambda_init]], np.float32),
        'consts': const_mats(), 'poolA': pool_matrices(r),
        'poolw': np.ascontiguousarray(inp['pool_w'][l][2 * r:2 * r + 2]), 'pscale': np.ascontiguousarray(inp['pool_scale'][l][None, 128 * r:128 * r + 128]),
        'lamv': np.concatenate([inp['lam_q1'][l], inp['lam_k1'][l], inp['lam_q2'][l], inp['lam_k2'][l]])[None, :].astype(np.float32),
        'gdiff': np.ascontiguousarray(inp['diff_norm_g'][l][None, :]),
        'convw': np.ascontiguousarray(inp['conv_w'][l][:, ch].T.reshape(4, 128, 4).transpose(1, 0, 2).reshape(128, 16)),
        'convb': np.ascontiguousarray(inp['conv_b'][l][ch].reshape(4, 128).T),
        'ssdv': np.concatenate([inp['dt_bias'][l][h4], inp['a_log'][l][h4], inp['d_skip'][l][h4]])[None, :].astype(np.float32),
    }


_PROGS = {}


def _prog(name, fn):
    if name not in _PROGS:
        _PROGS[name] = fn()
    return _PROGS[name]


def _run(kb, ins):
    return run_bass_kernel_spmd(kb.nc, ins, core_ids=list(range(8))).results


def kernel_unfused(**inp):
    inp = {k: np.asarray(v) for k, v in inp.items()}
    ident = np.eye(128, dtype=np.float32)
    hcur = [None] * 8
    for l in range(DEPTH):
        first = l == 0
        kbA = _prog('A1' if first else 'A2', lambda: build_A(first))
        ins = []
        for c in range(8):
            b, r = divmod(c, 2)
            m = {'hin': np.ascontiguousarray(inp['x'][b, r * TOK:(r + 1) * TOK]) if first else hcur[c],
                 'w_in': np.ascontiguousarray(inp['w_in'][l][:, perm_cols(r)]), 'ident': ident}
            if first:
                m['lng'] = inp['ln_in_g'][None, :]
                m['lnb'] = inp['ln_in_b'][None, :]
            ins.append(m)
        resA = _run(kbA, ins)
        hres = [resA[c]['hout'] for c in range(8)] if first else hcur
        kbB = _prog('B', lambda: build_B(NCH))
        ins = []
        for c in range(8):
            b, r = divmod(c, 2)
            parts, dparts = [], []
            for s in range(2):
                src = resA[2 * b + s]
                parts.append(src['L'] if s == r else src['S'])
                dparts.append(src['dt'] if s == r else src['dt'][:, [4, 5, 6, 7, 0, 1, 2, 3]])
            m = {'F': np.concatenate(parts, axis=0), 'Fdt': np.ascontiguousarray(np.concatenate(dparts, axis=0))}
            m.update(mixer_params(inp, l, r))
            ins.append(m)
        resB = _run(kbB, ins)
        kbC = _prog('C', lambda: build_C())
        ins = []
        for c in range(8):
            b, s = divmod(c, 2)
            sl = slice(s * TOK, (s + 1) * TOK)
            ins.append({'O': np.ascontiguousarray(np.concatenate([resB[2 * b]['OB'][sl], resB[2 * b + 1]['OB'][sl]], axis=1)),
                        'hres': hres[c], 'p': np.ascontiguousarray(inp['p'][l, b, sl]),
                        'w_out': np.ascontiguousarray(inp['w_out'][l][O_ROWS]), 'ssdg': inp['ssd_norm_g'][l][None, :],
                        'ln1g': inp['ln1_g'][l][None], 'ln1b': inp['ln1_b'][l][None], 'w_ff1': inp['w_ff1'][l], 'w_ff2': inp['w_ff2'][l],
                        'wg': inp['w_ple_gate'][l], 'wp': inp['w_ple'][l], 'ln2g': inp['ln2_g'][l][None], 'ln2b': inp['ln2_b'][l][None], 'ident': ident})
        resC = _run(kbC, ins)
        hcur = [resC[c]['hout'] for c in range(8)]
    out = np.empty((NB, SEQ, D), np.float32)
    for c in range(8):
        b, r = divmod(c, 2)
        out[b, r * TOK:(r + 1) * TOK] = hcur[c]
    return out


def kernel(**inp):
    return kernel_unfused(**inp)
```
